# Optimizing a Trainium2 kernel written in Bass

```python
import math
import jax, jax.numpy as jnp
from jax import lax
import numpy as np

D_MODEL = 1024
BATCH = 2
SEQ = 16384
DEPTH = 2

N_A_LAYERS = DEPTH // 2
N_B_LAYERS = DEPTH - N_A_LAYERS
CONV_WIDTH = 3
N_HEADS = 8
HEAD_DIM = 64
ATT_WIDTH = N_HEADS * HEAD_DIM
Q_BLOCK = 128
RMS_EPS = 1e-6

kernel_name = "yoco_shortconv_stickbreaking"


def _rmsnorm(x, g):
    xf = x.astype(jnp.float32)
    y = xf * lax.rsqrt(jnp.mean(xf * xf, axis=-1, keepdims=True) + RMS_EPS)
    return (y * g.astype(jnp.float32)).astype(x.dtype)


def _causal_depthwise_conv(u, w):
    return lax.conv_general_dilated(
        u, w[:, None, :].astype(u.dtype),
        window_strides=(1,),
        padding=[(CONV_WIDTH - 1, 0)],
        dimension_numbers=("NWC", "WIO", "NWC"),
        feature_group_count=u.shape[-1])


def _short_conv_mixer(h, w_in, w_conv, w_out):
    u = h @ w_in
    b_gate, c_gate, xin, g = jnp.split(u, 4, axis=-1)
    y = b_gate * _causal_depthwise_conv(c_gate * xin, w_conv)
    return (y * jax.nn.silu(g)) @ w_out


def _stick_breaking_attention(q, k, v):
    seq = q.shape[2]
    n_blocks = seq // Q_BLOCK
    scale = HEAD_DIM ** -0.5
    qf = q.astype(jnp.float32)
    kf = k.astype(jnp.float32)
    vf = v.astype(jnp.float32)
    outs = []
    for blk in range(n_blocks):
        q0 = blk * Q_BLOCK
        kl = q0 + Q_BLOCK
        qb = qf[:, :, q0:kl]
        z = jnp.einsum("bhqd,bhkd->bhqk", qb, kf[:, :, :kl]) * scale
        q_pos = q0 + jnp.arange(Q_BLOCK)
        mask = jnp.arange(kl)[None, :] < q_pos[:, None]
        log_one_minus = jnp.where(mask, jax.nn.log_sigmoid(-z), 0.0)
        rev = lax.cumsum(log_one_minus, axis=3, reverse=True)
        weights = jnp.exp(jnp.where(mask, z + rev, -jnp.inf))
        outs.append(jnp.einsum("bhqk,bhkd->bhqd", weights, vf[:, :, :kl]))
    o = jnp.concatenate(outs, axis=2)
    return o.astype(q.dtype)


def setup_inputs(seed: int = 0) -> dict:
    key = jax.random.key(seed)
    ks = jax.random.split(key, 12)
    D = D_MODEL
    s = D ** -0.5
    f32 = jnp.float32
    return {
        "x": jax.random.normal(ks[0], (BATCH, SEQ, D), f32),
        "norm_a": 1.0 + 0.02 * jax.random.normal(ks[1], (N_A_LAYERS, D), f32),
        "w_in_a": s * jax.random.normal(ks[2], (N_A_LAYERS, D, 4 * D), f32),
        "conv_a": (CONV_WIDTH ** -0.5) * jax.random.normal(ks[3], (N_A_LAYERS, CONV_WIDTH, D), f32),
        "w_out_a": s * jax.random.normal(ks[4], (N_A_LAYERS, D, D), f32),
        "norm_kv": 1.0 + 0.02 * jax.random.normal(ks[5], (D,), f32),
        "w_kv": s * jax.random.normal(ks[6], (D, 2 * ATT_WIDTH), f32),
        "norm_b": 1.0 + 0.02 * jax.random.normal(ks[7], (N_B_LAYERS, D), f32),
        "w_in_b": s * jax.random.normal(ks[8], (N_B_LAYERS, D, 2 * ATT_WIDTH), f32),
        "w_out_b": (ATT_WIDTH ** -0.5) * jax.random.normal(ks[9], (N_B_LAYERS, ATT_WIDTH, D), f32),
        "norm_f": 1.0 + 0.02 * jax.random.normal(ks[10], (D,), f32),
    }


def reference(x, norm_a, w_in_a, conv_a, w_out_a, norm_kv, w_kv, norm_b, w_in_b, w_out_b, norm_f):
    bsz, seq, _ = x.shape
    k = v = None
    for layer in range(DEPTH):
        if layer < N_A_LAYERS:
            i = layer
            h = _rmsnorm(x, norm_a[i])
            x = x + _short_conv_mixer(h, w_in_a[i], conv_a[i], w_out_a[i])
            if layer == N_A_LAYERS - 1:
                kv = _rmsnorm(x, norm_kv) @ w_kv
                k, v = jnp.split(kv, 2, axis=-1)
                k = k.reshape(bsz, seq, N_HEADS, HEAD_DIM).transpose(0, 2, 1, 3)
                v = v.reshape(bsz, seq, N_HEADS, HEAD_DIM).transpose(0, 2, 1, 3)
        else:
            j = layer - N_A_LAYERS
            h = _rmsnorm(x, norm_b[j])
            q, g = jnp.split(h @ w_in_b[j], 2, axis=-1)
            q = q.reshape(bsz, seq, N_HEADS, HEAD_DIM).transpose(0, 2, 1, 3)
            o = _stick_breaking_attention(q, k, v)
            o = o.transpose(0, 2, 1, 3).reshape(bsz, seq, ATT_WIDTH)
            x = x + (o * jax.nn.silu(g)) @ w_out_b[j]
    return _rmsnorm(x, norm_f)
```

```python
import numpy as np
import ml_dtypes
from concourse.bass_utils import run_bass_kernel_spmd

from contextlib import ExitStack

import concourse.bass as bass
import concourse.mybir as mybir

F32 = mybir.dt.float32
BF16 = mybir.dt.bfloat16
AF = mybir.ActivationFunctionType
ALU = mybir.AluOpType
AX = mybir.AxisListType

ENGS = ("pe", "act", "dve", "pool", "sp")


class Op:
    __slots__ = ("idx", "eng", "fn", "deps", "dsem", "sig", "cnt", "is_dma")

    def __init__(self, idx, eng, fn, deps, dsem):
        self.idx = idx
        self.eng = eng
        self.fn = fn
        self.deps = deps
        self.dsem = dsem
        self.is_dma = dsem is not None
        self.sig = False
        self.cnt = 0


class Prog:
    def __init__(self, nc):
        self.nc = nc
        self.ops = []
        self.last_w = {}
        self.readers = {}
        self.stack = ExitStack()
        self.n_sb = 0

    def sb(self, name, shape, dt):
        return self.stack.enter_context(self.nc.sbuf_tensor(name, list(shape), dt))

    def ps(self, name, shape, dt):
        return self.stack.enter_context(self.nc.psum_tensor(name, list(shape), dt))

    def add(self, eng, fn, r=(), w=(), dsem=None):
        idx = len(self.ops)
        deps = set()
        for k in r:
            j = self.last_w.get(k)
            if j is not None:
                deps.add(j)
        for k in w:
            j = self.last_w.get(k)
            if j is not None:
                deps.add(j)
            deps.update(self.readers.get(k, ()))
        for k in r:
            self.readers.setdefault(k, []).append(idx)
        for k in w:
            self.last_w[k] = idx
            self.readers[k] = []
        deps.discard(idx)
        op = Op(idx, eng, fn, deps, dsem)
        self.ops.append(op)
        return op

    def dma(self, out, in_, dsem, r=(), w=(), q="sp", **kw):
        return self.add(q, lambda e: e.dma_start(out=out, in_=in_, **kw), r, w, dsem=dsem)

    def emit(self, final_wait=True):
        nc = self.nc
        ops = self.ops
        for op in ops:
            for j in op.deps:
                d = ops[j]
                if d.is_dma:
                    continue
                if d.eng == op.eng and d.eng == "pe" and not op.is_dma:
                    continue
                d.sig = True
        ecnt = {e: 0 for e in ENGS}
        dcnt = {}
        for op in ops:
            if op.is_dma:
                dcnt[op.dsem] = dcnt.get(op.dsem, 0) + 16
                op.cnt = dcnt[op.dsem]
            elif op.sig:
                ecnt[op.eng] += 1
                op.cnt = ecnt[op.eng]
        esem = {e: self.stack.enter_context(nc.semaphore("s_" + e)) for e in ENGS if e != "sp"}
        dsem = {k: self.stack.enter_context(nc.semaphore("d_" + str(k))) for k in dcnt}
        self.n_sems = len(esem) + len(dsem)
        per_eng = {e: [op for op in ops if op.eng == e] for e in ENGS}
        finals = dict(dcnt)

        def run(engname, eng):
            known = {}
            for op in per_eng[engname]:
                need = {}
                for j in op.deps:
                    d = ops[j]
                    if d.is_dma:
                        key = ("d", d.dsem)
                    else:
                        if d.eng == engname and engname == "pe" and not op.is_dma:
                            continue
                        key = ("e", d.eng)
                    if d.cnt > need.get(key, 0):
                        need[key] = d.cnt
                for key, v in need.items():
                    if known.get(key, 0) >= v:
                        continue
                    known[key] = v
                    s = dsem[key[1]] if key[0] == "d" else esem[key[1]]
                    eng.wait_ge(s, v)
                ins = op.fn(eng)
                if op.is_dma:
                    ins.then_inc(dsem[op.dsem], 16)
                elif op.sig:
                    ins.then_inc(esem[op.eng], 1)
            if engname == "sp" and final_wait:
                for k, v in finals.items():
                    eng.wait_ge(dsem[k], v)

        with nc.Block() as block:
            @block.tensor
            def _(e):
                run("pe", e)

            @block.scalar
            def _(e):
                run("act", e)

            @block.vector
            def _(e):
                run("dve", e)

            @block.gpsimd
            def _(e):
                run("pool", e)

            @block.sync
            def _(e):
                run("sp", e)
        self.stack.close()


def _mk(method, **kw):
    return lambda e: getattr(e, method)(**kw)


def _h_mm(self, out, lhsT, rhs, start, stop, r, w, **kw):
    return self.add("pe", _mk("matmul", out=out, lhsT=lhsT, rhs=rhs, start=start, stop=stop, **kw), r, w)


def _h_tr(self, out, in_, identity, r, w):
    return self.add("pe", _mk("transpose", out=out, in_=in_, identity=identity), r, w)


def _h_act(self, out, in_, func, r, w, **kw):
    return self.add("act", _mk("activation", out=out, in_=in_, func=func, **kw), r, w)


def _h_op(self, eng, method, r, w, **kw):
    return self.add(eng, _mk(method, **kw), r, w)


Prog.mm = _h_mm
Prog.tr = _h_tr
Prog.act = _h_act
Prog.op = _h_op


D = 1024
NSLOT = 4
CH = 1024
KV_ROWS = 4098


class Ring:
    def __init__(self, P, name, n, shape, dt):
        self.t = [P.sb(f"{name}{i}", shape, dt) for i in range(n)]
        self.k = [f"{name}{i}" for i in range(n)]
        self.i = 0

    def next(self):
        i = self.i
        self.i = (i + 1) % len(self.t)
        return self.t[i], self.k[i]


def phase_a(nc, io):
    P = Prog(nc)
    xin, x1d, kvl, qtd, sgd, vtd = io["xin"], io["x1"], io["kvl"], io["qt"], io["sg"], io["vt"]
    dvv = kvl[2048:4096, :].rearrange("(s h pa) (pb kb d) -> s h (pa pb) kb d", s=4, h=8, pa=64, pb=2, kb=8, d=64)

    ident = P.sb("ident", [128, 128], BF16)
    P.add("pool", lambda e: e.memset(ident[:], 0.0), w=["ident"])
    P.add("pool", lambda e: e.affine_select(out=ident[:], in_=ident[:], compare_op=ALU.not_equal, fill=1.0,
                                            base=0, pattern=[[-1, 128]], channel_multiplier=1), r=["ident"], w=["ident"])
    gains = P.sb("gains", [128, 3, 8], F32)
    cw = P.sb("cw", [128, 3, 8], F32)
    for i, nm in enumerate(["norm_a", "norm_kv", "norm_b"]):
        P.dma(gains[:, i, :], io[nm].rearrange("(kc p) -> p kc", p=128), "cst_g", w=["gains"], allow_slow_non_contiguous=True)
    for k in range(3):
        P.dma(cw[:, k, :], io["conv_a"][k, :].rearrange("(cc p) -> p cc", p=128), "cst_c", w=["cw"], allow_slow_non_contiguous=True)

    wa = P.sb("wa", [128, 8, 4096], BF16)
    wo = P.sb("wo", [128, 8, 1024], BF16)
    w2 = wa
    wst = Ring(P, "wst", 2, [128, 1024], F32)
    cvt_i = [0]

    def load_w(dst, dkey, col0, wdram, ncols, gi):
        for kc in range(8):
            for c0 in range(0, ncols, 1024):
                w = min(1024, ncols - c0)
                st, sk = wst.next()
                P.dma(st[:, :w], wdram[kc * 128:(kc + 1) * 128, c0:c0 + w], sk, w=[sk])
                eng = "dve" if cvt_i[0] % 2 == 0 else "pool"
                cvt_i[0] += 1
                o = dst[:, kc, col0 + c0:col0 + c0 + w]
                if gi is None:
                    P.add(eng, lambda e, o=o, st=st, w=w: e.tensor_copy(out=o, in_=st[:, :w]), r=[sk], w=[dkey])
                else:
                    g = gains[:, gi, kc:kc + 1]
                    P.add(eng, lambda e, o=o, st=st, w=w, g=g: e.tensor_scalar(out=o, in0=st[:, :w], scalar1=g, scalar2=None, op0=ALU.mult),
                          r=[sk, "gains"], w=[dkey])

    load_w(wa, "wa", 0, io["w_in_a"], 4096, 0)
    load_w(wo, "wo", 0, io["w_out_a"], 1024, None)

    xres = Ring(P, "xres", 2, [128, 4, 1024], F32)
    hb = Ring(P, "hb", 2, [128, 1024], BF16)
    junk = P.sb("junk", [128, 1024], BF16)
    ssr = Ring(P, "ss", 4, [128, 1], F32)
    rsr = Ring(P, "rs", 4, [128, 1], F32)
    hT = Ring(P, "hT", 2, [128, 8, 514], BF16)
    csb = Ring(P, "csb", 2, [128, 512], F32)
    cxb = Ring(P, "cxb", 3, [128, 514], F32)
    cvb = Ring(P, "cvb", 2, [128, 512], F32)
    eb = Ring(P, "eb", 2, [128, 512], F32)
    t1b = Ring(P, "t1b", 2, [128, 512], F32)
    carry = P.sb("carry", [128, 8, 2], F32)
    hsm = P.sb("hsm", [128, 4], F32)
    yT = P.sb("yT", [128, 8, 512], BF16)
    dh = P.sb("dh", [128, 8, 512], BF16)
    prevc = P.sb("prevc", [128, 8, 1], BF16)
    stg = Ring(P, "stg", 4, [128, 512], BF16)
    dvb = Ring(P, "dvb", 2, [128, 8, 4, 64], BF16)
    vlb = P.sb("vlb", [1, 512], BF16)
    xhalo = P.sb("xhalo", [2, 1024], F32)

    pT = P.ps("pT", [128, 1024], BF16)
    pin = [P.ps(f"pin{i}", [128, 512], F32) for i in range(4)]
    pinK = [f"pin{i}" for i in range(4)]
    po = [P.ps(f"po{i}", [128, 512], F32) for i in range(2)]
    poK = ["po0", "po1"]
    ph = P.ps("ph", [128, 512], F32)

    def stats_and_T(xb, xk, npart, dstT, dstk, col0):
        ss, ssk = ssr.next()
        rs, rsk = rsr.next()
        h, hk = hb.next()
        P.act(junk[:npart, :], xb, AF.Square, [xk], ["junk", ssk], accum_out=ss[:npart, :])
        P.act(rs[:npart, :], ss[:npart, :], AF.Ln, [ssk], [rsk], scale=1.0 / D, bias=1e-6)
        P.act(rs[:npart, :], rs[:npart, :], AF.Exp, [rsk], [rsk], scale=-0.5)
        P.op("dve", "tensor_scalar", [xk, rsk], [hk], out=h[:npart, :], in0=xb, scalar1=rs[:npart, :], scalar2=None, op0=ALU.mult)
        for kc in range(8):
            P.tr(pT[:, kc * 128:kc * 128 + npart], h[:npart, kc * 128:(kc + 1) * 128], ident[:npart, :npart], [hk, "ident"], ["pT"])
        pv = pT[:].rearrange("p (kc t) -> p kc t", kc=8)[:, :, 0:npart]
        P.op("act", "copy", ["pT"], [dstk], out=dstT[:, :, col0:col0 + npart], in_=pv)

    for s in range(NSLOT):
        for tt in range(2):
            xr, xk = xres.next()
            hTt, hTk = hT.next()
            row0 = 2 + tt * 512
            for sub in range(4):
                P.dma(xr[:, sub, :], xin[s, row0 + sub * 128:row0 + (sub + 1) * 128, :], xk + "_ld", w=[xk])
            for sub in range(4):
                stats_and_T(xr[:, sub, :], xk, 128, hTt, hTk, 2 + sub * 128)
            if tt == 0:
                P.dma(xhalo[:], xin[s, 0:2, :], "xh", w=["xhalo"])
                stats_and_T(xhalo[:], "xhalo", 2, hTt, hTk, 0)
            for i in range(8):
                order = [(0, 8 + i), (1, 16 + i), (2, 24 + i), (3, i)]
                if tt == 0:
                    for (bi, mc) in order[:2]:
                        for kc in range(8):
                            P.mm(ph[:, bi * 2:bi * 2 + 2], wa[:, kc, mc * 128:(mc + 1) * 128], hTt[:, kc, 0:2], kc == 0, kc == 7, ["wa", hTk], ["ph"])
                    P.op("act", "copy", ["ph"], ["hsm"], out=hsm[:, 0:4], in_=ph[:, 0:4])
                for (bi, mc) in order:
                    for kc in range(8):
                        P.mm(pin[bi][:], wa[:, kc, mc * 128:(mc + 1) * 128], hTt[:, kc, 2:514], kc == 0, kc == 7, ["wa", hTk], [pinK[bi]])
                cs, csk = csb.next()
                cx, cxk = cxb.next()
                cv, cvk = cvb.next()
                ee, ek = eb.next()
                t1, t1k = t1b.next()
                P.op("act", "copy", [pinK[0]], [csk], out=cs[:], in_=pin[0][:])
                P.op("dve", "tensor_tensor", [csk, pinK[1]], [cxk], out=cx[:, 2:514], in0=cs[:], in1=pin[1][:], op=ALU.mult)
                if tt == 0:
                    P.op("dve", "tensor_tensor", ["hsm"], [cxk], out=cx[:, 0:2], in0=hsm[:, 0:2], in1=hsm[:, 2:4], op=ALU.mult)
                else:
                    P.op("pool", "tensor_copy", ["carry"], [cxk], out=cx[:, 0:2], in_=carry[:, i, :])
                P.op("pool", "tensor_copy", [cxk], ["carry"], out=carry[:, i, :], in_=cx[:, 512:514])
                P.op("pool", "tensor_scalar", [cxk, "cw"], [cvk], out=cv[:], in0=cx[:, 0:512], scalar1=cw[:, 0, i:i + 1], scalar2=None, op0=ALU.mult)
                P.op("dve", "scalar_tensor_tensor", [cxk, "cw", cvk], [cvk], out=cv[:], in0=cx[:, 1:513], scalar=cw[:, 1, i:i + 1], in1=cv[:],
                     op0=ALU.mult, op1=ALU.add)
                P.op("dve", "scalar_tensor_tensor", [cxk, "cw", cvk], [cvk], out=cv[:], in0=cx[:, 2:514], scalar=cw[:, 2, i:i + 1], in1=cv[:],
                     op0=ALU.mult, op1=ALU.add)
                P.act(ee[:], pin[2][:], AF.Exp, [pinK[2]], [ek], scale=-1.0)
                P.op("pool", "tensor_scalar", [ek], [ek], out=ee[:], in0=ee[:], scalar1=1.0, scalar2=None, op0=ALU.add)
                P.op("dve", "reciprocal", [ek], [ek], out=ee[:], in_=ee[:])
                P.op("dve", "tensor_tensor", [ek, pinK[2]], [t1k], out=t1[:], in0=ee[:], in1=pin[2][:], op=ALU.mult)
                P.op("dve", "tensor_tensor", [cvk, pinK[3]], [cvk], out=cv[:], in0=cv[:], in1=pin[3][:], op=ALU.mult)
                P.op("pool", "tensor_tensor", [cvk, t1k], ["yT"], out=yT[:, i, :], in0=cv[:], in1=t1[:], op=ALU.mult)
            for sub in range(4):
                for half in range(2):
                    for kc in range(8):
                        P.mm(po[half][:], yT[:, kc, sub * 128:(sub + 1) * 128], wo[:, kc, half * 512:(half + 1) * 512], kc == 0, kc == 7,
                             ["yT", "wo"], [poK[half]])
                    xs = xr[:, sub, half * 512:(half + 1) * 512]
                    P.op("dve", "tensor_tensor", [xk, poK[half]], [xk], out=xs, in0=xs, in1=po[half][:], op=ALU.add)
                t0 = tt * 512 + sub * 128
                P.dma(x1d[s, t0:t0 + 128, :], xr[:, sub, :], xk + "_st", r=[xk], w=[("x1d", s, tt)])

    load_w(w2, "wa", 0, io["w_kv"], 1024, 1)
    load_w(w2, "wa", 1024, io["w_in_b"], 1024, 2)
    zc = P.sb("zc", [128, 8, 1], BF16)
    P.op("pool", "memset", [], ["zc"], ap=zc[:], constant=0.0)
    pmi = [0]

    def nextbank():
        i = pmi[0]
        pmi[0] = (i + 1) % 4
        return pin[i], pinK[i]

    for s in range(NSLOT):
        for tt in range(2):
            xr, xk = xres.next()
            hTt, hTk = hT.next()
            for sub in range(4):
                t0 = tt * 512 + sub * 128
                P.dma(xr[:, sub, :], x1d[s, t0:t0 + 128, :], xk + "_ld", r=[("x1d", s, tt)], w=[xk])
            if tt == 0:
                P.op("pool", "tensor_copy", ["zc"], [hTk], out=hTt[:, :, 1:2], in_=zc[:])
            else:
                P.op("pool", "tensor_copy", ["prevc"], [hTk], out=hTt[:, :, 1:2], in_=prevc[:])
            for sub in range(4):
                stats_and_T(xr[:, sub, :], xk, 128, hTt, hTk, 2 + sub * 128)
            P.op("pool", "tensor_copy", [hTk], ["prevc"], out=prevc[:], in_=hTt[:, :, 513:514])
            P.op("dve", "tensor_tensor", [hTk], ["dh"], out=dh[:], in0=hTt[:, :, 1:513], in1=hTt[:, :, 2:514], op=ALU.subtract)
            for which in range(4):
                c0 = [0, 1024, 1536, 512][which]
                for mc in range(4):
                    bank, bk = nextbank()
                    c = c0 + mc * 128
                    for kc in range(8):
                        P.mm(bank[:], w2[:, kc, c:c + 128], hTt[:, kc, 2:514], kc == 0, kc == 7, ["wa", hTk], [bk])
                    st, sk = stg.next()
                    if which == 3:
                        P.op("act", "copy", [bk], [sk], out=st[:], in_=bank[:])
                        P.dma(vtd[s, mc * 128:(mc + 1) * 128, tt * 512:(tt + 1) * 512], st[:], sk + "_st", r=[sk])
                    elif which == 0:
                        P.op("act", "copy", [bk], [sk], out=st[:], in_=bank[:])
                        P.dma(kvl[s * 512 + mc * 128:s * 512 + (mc + 1) * 128, tt * 512:(tt + 1) * 512], st[:], sk + "_st", r=[sk])
                    elif which == 1:
                        P.op("act", "mul", [bk], [sk], out=st[:], in_=bank[:], mul=0.125)
                        P.dma(qtd[s, mc * 128:(mc + 1) * 128, tt * 512:(tt + 1) * 512], st[:], sk + "_st", r=[sk])
                    else:
                        ee, ek = eb.next()
                        P.act(ee[:], bank[:], AF.Exp, [bk], [ek], scale=-1.0)
                        P.op("pool", "tensor_scalar", [ek], [ek], out=ee[:], in0=ee[:], scalar1=1.0, scalar2=None, op0=ALU.add)
                        P.op("dve", "reciprocal", [ek], [ek], out=ee[:], in_=ee[:])
                        P.op("dve", "tensor_tensor", [ek, bk], [sk], out=st[:], in0=ee[:], in1=bank[:], op=ALU.mult)
                        P.dma(sgd[s, mc * 128:(mc + 1) * 128, tt * 512:(tt + 1) * 512], st[:], sk + "_st", r=[sk])
            dv, dvk = dvb.next()
            for sub in range(4):
                bank, bk = po[sub % 2], poK[sub % 2]
                for kc in range(8):
                    P.mm(bank[:], dh[:, kc, sub * 128:(sub + 1) * 128], w2[:, kc, 512:1024], kc == 0, kc == 7, ["dh", "wa"], [bk])
                P.op("dve", "tensor_copy", [bk], [dvk], out=dv[:, :, sub, :], in_=bank[:].rearrange("p (h d) -> p h d", h=8))
            P.dma(dvv[s][:, :, tt * 4:(tt + 1) * 4, :].rearrange("h p kb d -> p h kb d"), dv[:], dvk + "_st", r=[dvk])
            if tt == 1:
                for kc in range(8):
                    P.mm(ph[0:1, :], hTt[:, kc, 513:514], w2[:, kc, 512:1024], kc == 0, kc == 7, [hTk, "wa"], ["ph"])
                P.op("act", "copy", ["ph"], ["vlb"], out=vlb[:], in_=ph[0:1, :])
                P.dma(kvl[4096 + s // 2:4097 + s // 2, (s % 2) * 512:(s % 2 + 1) * 512], vlb[:], "vl_st", r=["vlb"])
    return P


S = 16384
NQB = S // 512


def phase_b(nc, io, nqb=NQB):
    P = Prog(nc)
    ktd, vtd, qtd, dvd, vld, otd = io["kt2"], io["vt2"], io["qt2"], io["dv2"], io["vl2"], io["ot2"]
    kt = P.sb("kt", [128, S], BF16)
    vt = P.sb("vt", [128, S + 2], BF16)
    dv = P.sb("dv", [128, 128, 128], BF16)
    vl = P.sb("vl", [1, 16, 128], BF16)
    tri = P.sb("tri", [128, 128], BF16)
    comp = P.sb("comp", [128, 128], BF16)
    mk = [P.sb(f"mk{o}", [128, 2, 512], BF16) for o in range(4)]
    P.op("pool", "memset", [], ["tri"], ap=tri[:], constant=-1.0)
    P.op("pool", "affine_select", ["tri"], ["tri"], out=tri[:], in_=tri[:], compare_op=ALU.is_ge, fill=0.0, base=0,
         pattern=[[-1, 128]], channel_multiplier=1)
    P.op("pool", "memset", [], ["comp"], ap=comp[:], constant=-1.0)
    P.op("pool", "affine_select", ["comp"], ["comp"], out=comp[:], in_=comp[:], compare_op=ALU.is_gt, fill=0.0, base=0,
         pattern=[[1, 128]], channel_multiplier=-1)
    for o in range(4):
        P.op("pool", "memset", [], [f"mk{o}"], ap=mk[o][:], constant=1.0)
        P.op("pool", "affine_select", [f"mk{o}"], [f"mk{o}"], out=mk[o][:], in_=mk[o][:], compare_op=ALU.is_gt, fill=0.0,
             base=-128 * o, pattern=[[0, 2], [1, 512]], channel_multiplier=-1)
    P.op("pool", "memset", [], [("vt", 0)], ap=vt[:, 0:2], constant=0.0)
    P.dma(vl[:], vld, "vl_ld", w=["vl"])
    for j in range(16):
        P.dma(kt[:, j * 1024:(j + 1) * 1024], ktd[:, j * 1024:(j + 1) * 1024], ("kt", j), w=[("kt", j)])
        P.dma(dv[:, j * 8:(j + 1) * 8, :], dvd[:, j * 8:(j + 1) * 8, :], ("dv", j), w=[("dv", j)])
        P.dma(vt[:, 2 + j * 1024:2 + (j + 1) * 1024], vtd[:, j * 1024:(j + 1) * 1024], ("vt", j), w=[("vt", j)] if j else [("vt", 0)])
        if j >= 1:
            P.op("dve", "tensor_tensor", [("dv", j), "vl"], [("dv", j)], out=dv[0:1, j * 8, :], in0=dv[0:1, j * 8, :], in1=vl[0:1, j - 1, :], op=ALU.add)

    qb_ring = Ring(P, "qb", 3, [128, 512], BF16)
    ob_ring = Ring(P, "ob", 2, [128, 512], BF16)
    Eb = Ring(P, "E", 2, [128, 1024], F32)
    Lb = Ring(P, "L", 2, [128, 1024], BF16)
    Xb = Ring(P, "X", 2, [128, 1024], BF16)
    Z = [P.ps(f"Z{i}", [128, 1024], F32) for i in range(2)]
    PP = P.ps("PP", [128, 1024], F32)
    Ops = P.ps("O", [128, 512], F32)

    steps = []
    for qb in range(nqb):
        for kb in range(4 * qb + 3, -1, -1):
            steps.append((qb, kb))
    n = len(steps)
    qtiles = {}
    st = {}

    def front(k):
        qb, kb = steps[k]
        first = kb == 4 * qb + 3
        if first:
            q, qk = qb_ring.next()
            P.dma(q[:], qtd[:, qb * 512:(qb + 1) * 512], qk + "_ld", w=[qk])
            qtiles[qb] = (q, qk)
        q, qk = qtiles[qb]
        z, zk = Z[k % 2], f"Z{k % 2}"
        E, Ek = Eb.next()
        L, Lk = Lb.next()
        kj = ("kt", kb // 8)
        for hh in range(2):
            P.mm(z[:, hh * 512:(hh + 1) * 512], kt[hh * 64:(hh + 1) * 64, kb * 128:(kb + 1) * 128], q[hh * 64:(hh + 1) * 64, :], True, True,
                 [kj, qk], [zk])
        P.act(E[:], z[:], AF.Exp, [zk], [Ek])
        P.act(L[:], E[:], AF.Ln, [Ek], [Lk], bias=1.0)
        o = kb - 4 * qb
        if o >= 0:
            P.op("pool", "tensor_tensor", [Lk, f"mk{o}"], [Lk], out=L[:], in0=L[:], in1=mk[o][:].rearrange("p a b -> p (a b)"), op=ALU.mult)
        st[k] = (L, Lk)

    def tri_mm(k):
        qb, kb = steps[k]
        first = kb == 4 * qb + 3
        L, Lk = st[k]
        for hh in range(2):
            P.mm(PP[:, hh * 512:(hh + 1) * 512], tri[:], L[:, hh * 512:(hh + 1) * 512], first, False, ["tri", Lk], ["PP"], skip_group_check=True)

    def back(k):
        qb, kb = steps[k]
        first = kb == 4 * qb + 3
        last = kb == 0
        L, Lk = st.pop(k)
        X, Xk = Xb.next()
        P.act(X[:], PP[:], AF.Exp, ["PP"], [Xk])
        o = kb - 4 * qb
        if o >= 0:
            P.op("pool", "tensor_tensor", [Xk, f"mk{o}"], [Xk], out=X[:], in0=X[:], in1=mk[o][:].rearrange("p a b -> p (a b)"), op=ALU.mult)
        if not last:
            for hh in range(2):
                P.mm(PP[:, hh * 512:(hh + 1) * 512], comp[:], L[:, hh * 512:(hh + 1) * 512], False, False, ["comp", Lk], ["PP"], skip_group_check=True)
        if k + 1 < n:
            tri_mm(k + 1)
        dj = ("dv", kb // 8)
        for hh in range(2):
            P.mm(Ops[hh * 64:(hh + 1) * 64, :], dv[:, kb, hh * 64:(hh + 1) * 64], X[:, hh * 512:(hh + 1) * 512], first, last, [dj, Xk], ["O"],
                 skip_group_check=True)
        if last:
            ob, obk = ob_ring.next()
            q0 = qb * 512
            vkeys = [("vt", j) for j in sorted({max(q0 - 1, 0) // 1024, (q0 + 510) // 1024})]
            P.op("dve", "tensor_tensor", ["O"] + vkeys, [obk], out=ob[:], in0=Ops[:], in1=vt[:, 1 + q0:1 + q0 + 512], op=ALU.add)
            P.dma(otd[:, q0:q0 + 512], ob[:], obk + "_st", r=[obk])

    front(0)
    tri_mm(0)
    for k in range(n):
        if k + 1 < n:
            front(k + 1)
        back(k)
    return P


def phase_c(nc, io):
    P = Prog(nc)
    x1d, sgd, otd, outd = io["x1"], io["sg"], io["ot"], io["out"]
    wb = P.sb("wb", [128, 4, 1024], BF16)
    wst = Ring(P, "wst", 2, [128, 1024], F32)
    for kc in range(4):
        s_, sk = wst.next()
        P.dma(s_[:], io["w_out_b"][kc * 128:(kc + 1) * 128, :], sk, w=[sk])
        P.op("dve", "tensor_copy", [sk], ["wb"], out=wb[:, kc, :], in_=s_[:])
    gf = P.sb("gf", [128, 1024], F32)
    P.dma(gf[:], io["norm_f"].partition_broadcast(128), "gf", w=["gf"])
    xr = Ring(P, "xr", 3, [128, 1024], F32)
    sgb = Ring(P, "sgb", 2, [128, 4, 512], BF16)
    otb = Ring(P, "otb", 2, [128, 4, 512], BF16)
    ogb = Ring(P, "ogb", 2, [128, 4, 512], BF16)
    junk = P.sb("junk", [128, 1024], BF16)
    ssr = Ring(P, "ss", 4, [128, 1], F32)
    rsr = Ring(P, "rs", 4, [128, 1], F32)
    po = [P.ps(f"po{i}", [128, 512], F32) for i in range(4)]
    pi = [0]
    for s in range(4):
        for tt in range(2):
            sg, sgk = sgb.next()
            ot, otk = otb.next()
            og, ogk = ogb.next()
            P.dma(sg[:], sgd[s].rearrange("(kc p) t -> p kc t", p=128)[:, :, tt * 512:(tt + 1) * 512], sgk + "_ld", w=[sgk])
            P.dma(ot[:], otd[s].rearrange("(kc p) t -> p kc t", p=128)[:, :, tt * 512:(tt + 1) * 512], otk + "_ld", w=[otk])
            P.op("pool", "tensor_tensor", [sgk, otk], [ogk], out=og[:], in0=sg[:], in1=ot[:], op=ALU.mult)
            for sub in range(4):
                x, xk = xr.next()
                t0 = tt * 512 + sub * 128
                P.dma(x[:], x1d[s, t0:t0 + 128, :], xk + "_ld", w=[xk])
                for half in range(2):
                    b = pi[0]
                    pi[0] = (b + 1) % 4
                    for kc in range(4):
                        P.mm(po[b][:], og[:, kc, sub * 128:(sub + 1) * 128], wb[:, kc, half * 512:(half + 1) * 512], kc == 0, kc == 3,
                             [ogk, "wb"], [f"po{b}"])
                    xs = x[:, half * 512:(half + 1) * 512]
                    P.op("dve", "tensor_tensor", [xk, f"po{b}"], [xk], out=xs, in0=xs, in1=po[b][:], op=ALU.add)
                ss, ssk = ssr.next()
                rs, rsk = rsr.next()
                P.act(junk[:], x[:], AF.Square, [xk], ["junk", ssk], accum_out=ss[:])
                P.act(rs[:], ss[:], AF.Ln, [ssk], [rsk], scale=1.0 / 1024, bias=1e-6)
                P.act(rs[:], rs[:], AF.Exp, [rsk], [rsk], scale=-0.5)
                P.op("dve", "scalar_tensor_tensor", [xk, rsk, "gf"], [xk], out=x[:], in0=x[:], scalar=rs[:], in1=gf[:], op0=ALU.mult, op1=ALU.mult)
                P.dma(outd[s, t0:t0 + 128, :], x[:], xk + "_st", r=[xk])
    return P


BF = ml_dtypes.bfloat16
NCORES = 8


def _build(fn, ins, outs, **kw):
    nc = bass.Bass("TRN2", target_bir_lowering=False)
    io = {}
    for n, shape, dt in ins:
        io[n] = nc.dram_tensor(n, list(shape), dt, kind="ExternalInput").ap()
    for n, shape, dt in outs:
        io[n] = nc.dram_tensor(n, list(shape), dt, kind="ExternalOutput").ap()
    P = fn(nc, io, **kw)
    P.emit()
    return nc


def kernel(x, norm_a, w_in_a, conv_a, w_out_a, norm_kv, w_kv, norm_b, w_in_b, w_out_b, norm_f):
    x = np.asarray(x, np.float32)
    f32 = lambda a: np.ascontiguousarray(np.asarray(a, np.float32))
    cores = list(range(NCORES))
    wts = dict(w_in_a=f32(w_in_a[0]), conv_a=f32(conv_a[0]), w_out_a=f32(w_out_a[0]), w_kv=f32(w_kv), w_in_b=f32(w_in_b[0]),
               norm_a=f32(norm_a[0]), norm_kv=f32(norm_kv), norm_b=f32(norm_b[0]))
    in_a = []
    for c in cores:
        xin = np.zeros((4, 1026, 1024), np.float32)
        for s in range(4):
            g = 4 * c + s
            b, j = g // 16, g % 16
            lo = j * 1024 - 2
            if j == 0:
                xin[s, 2:] = x[b, 0:1024]
            else:
                xin[s] = x[b, lo:lo + 1026]
        in_a.append(dict(xin=xin, **wts))
    nc_a = _build(phase_a,
                  [("xin", [4, 1026, 1024], F32), ("w_in_a", [1024, 4096], F32), ("conv_a", [3, 1024], F32), ("w_out_a", [1024, 1024], F32),
                   ("w_kv", [1024, 1024], F32), ("w_in_b", [1024, 1024], F32), ("norm_a", [1024], F32), ("norm_kv", [1024], F32),
                   ("norm_b", [1024], F32)],
                  [("x1", [4, 1024, 1024], F32), ("kvl", [KV_ROWS, 1024], BF16), ("qt", [4, 512, 1024], BF16), ("sg", [4, 512, 1024], BF16),
                   ("vt", [4, 512, 1024], BF16)])
    ra = run_bass_kernel_spmd(nc_a, in_a, core_ids=cores).results
    KT = np.zeros((2, 512, 16384), BF); VT = np.zeros((2, 512, 16384), BF); QT = np.zeros((2, 512, 16384), BF)
    DV = np.zeros((2, 8, 128, 128, 64), BF)
    VL = np.zeros((2, 16, 512), BF)
    for c in cores:
        kvl = np.asarray(ra[c]["kvl"])
        for s in range(4):
            g = 4 * c + s
            b, j = g // 16, g % 16
            sl = slice(j * 1024, (j + 1) * 1024)
            KT[b][:, sl] = kvl[s * 512:(s + 1) * 512]
            QT[b][:, sl] = np.asarray(ra[c]["qt"])[s]
            VT[b][:, sl] = np.asarray(ra[c]["vt"])[s]
            DV[b][:, :, j * 8:(j + 1) * 8, :] = kvl[2048:4096].reshape(4, 8, 128, 8, 64)[s]
            VL[b, j] = kvl[4096 + s // 2, (s % 2) * 512:(s % 2 + 1) * 512]
    in_b = []
    for c in cores:
        b, hp = c // 4, c % 4
        fs = slice(hp * 128, (hp + 1) * 128)
        dv2 = np.ascontiguousarray(DV[b][2 * hp:2 * hp + 2].transpose(1, 2, 0, 3).reshape(128, 128, 128))
        in_b.append(dict(kt2=np.ascontiguousarray(KT[b][fs]), vt2=np.ascontiguousarray(VT[b][fs]), qt2=np.ascontiguousarray(QT[b][fs]),
                         dv2=dv2, vl2=np.ascontiguousarray(VL[b][:, fs].reshape(1, 16, 128))))
    nc_b = _build(phase_b,
                  [("kt2", [128, S], BF16), ("vt2", [128, S], BF16), ("qt2", [128, S], BF16), ("dv2", [128, 128, 128], BF16),
                   ("vl2", [1, 16, 128], BF16)],
                  [("ot2", [128, S], BF16)])
    rb = run_bass_kernel_spmd(nc_b, in_b, core_ids=cores).results
    OT = np.zeros((2, 512, 16384), BF)
    for c in cores:
        b, hp = c // 4, c % 4
        OT[b][hp * 128:(hp + 1) * 128] = np.asarray(rb[c]["ot2"])
    in_c = []
    for c in cores:
        ot = np.zeros((4, 512, 1024), BF)
        for s in range(4):
            g = 4 * c + s
            b, j = g // 16, g % 16
            ot[s] = OT[b][:, j * 1024:(j + 1) * 1024]
        in_c.append(dict(x1=np.asarray(ra[c]["x1"]), sg=np.asarray(ra[c]["sg"]), ot=ot, w_out_b=f32(w_out_b[0]), norm_f=f32(norm_f)))
    nc_c = _build(phase_c,
                  [("x1", [4, 1024, 1024], F32), ("sg", [4, 512, 1024], BF16), ("ot", [4, 512, 1024], BF16), ("w_out_b", [512, 1024], F32),
                   ("norm_f", [1024], F32)],
                  [("out", [4, 1024, 1024], F32)])
    rc = run_bass_kernel_spmd(nc_c, in_c, core_ids=cores).results
    out = np.zeros((2, 16384, 1024), np.float32)
    for c in cores:
        o = np.asarray(rc[c]["out"])
        for s in range(4):
            g = 4 * c + s
            b, j = g // 16, g % 16
            out[b, j * 1024:(j + 1) * 1024] = o[s]
    return out
```
